# Optimizing a Trainium2 kernel written in Bass

```python
import jax, jax.numpy as jnp
from jax import lax
import numpy as np

D_MODEL = 2048
BATCH = 2
SEQ = 4096
DEPTH = 1

GRID_W = 64
CTX_LEN = 256
HG_HEADS = 8
HG_DK = 128
HG_DV = 128
HG_WIDTH = HG_HEADS * HG_DV
HG_CHUNK = 32
ATT_HEADS = 8
ATT_KV_HEADS = 2
HEAD_DIM = 128
ATT_WIDTH = ATT_HEADS * HEAD_DIM
WINDOW = 128
ATT_BLOCK = 128
ROPE_THETA = 10000.0
D_MIX = HG_WIDTH + ATT_WIDTH
HG_COLS = 3 * HG_HEADS * HG_DK + 2 * HG_HEADS * HG_DV
ATT_COLS = (ATT_HEADS + 2 * ATT_KV_HEADS) * HEAD_DIM
IN_COLS = HG_COLS + ATT_COLS
D_FF = ((8 * D_MODEL + 3 * 256 - 1) // (3 * 256)) * 256
N_MOD = 6
RMS_EPS = 1e-6
NEG_INF = -1e30
LB_SLACK = 1.0

kernel_name = "hymba_style_hgrn2_window_gqa_dit_block"


def rms_norm(x, g):
    xf = x.astype(jnp.float32)
    y = xf * lax.rsqrt(jnp.mean(xf * xf, axis=-1, keepdims=True) + RMS_EPS)
    return (y * g.astype(jnp.float32)).astype(x.dtype)


def modulate(h, shift, scale):
    return h * (1 + scale) + shift


def axial_rope(x, row, col):
    half = HEAD_DIM // 2
    nf = half // 2
    inv_freq = ROPE_THETA ** (-jnp.arange(nf, dtype=jnp.float32) / nf)
    extra = (1,) * (x.ndim - 3)

    def rotate(xa, pos):
        ang = pos.astype(jnp.float32)[:, None] * inv_freq
        cos = jnp.cos(ang).reshape((1, -1) + extra + (nf,))
        sin = jnp.sin(ang).reshape((1, -1) + extra + (nf,))
        x1 = xa[..., :nf].astype(jnp.float32)
        x2 = xa[..., nf:].astype(jnp.float32)
        return jnp.concatenate([x1 * cos - x2 * sin, x1 * sin + x2 * cos], axis=-1)

    out = jnp.concatenate([rotate(x[..., :half], row), rotate(x[..., half:], col)], axis=-1)
    return out.astype(x.dtype)


def chunked_gated_linear_scan(q, k, v, log_f, s0):
    B, H, T, DK = q.shape
    DV = v.shape[-1]
    N = T // HG_CHUNK
    q = q.reshape(B, H, N, HG_CHUNK, DK)
    k = k.reshape(B, H, N, HG_CHUNK, DK)
    v = v.reshape(B, H, N, HG_CHUNK, DV)
    b = jnp.cumsum(log_f.reshape(B, H, N, HG_CHUNK, DK), axis=3)
    b_last = b[:, :, :, -1:, :]
    q_dec = q * jnp.exp(b)
    k_inv = k * jnp.exp(-b)
    causal_in_chunk = jnp.tril(jnp.ones((HG_CHUNK, HG_CHUNK), dtype=bool))
    scores = jnp.einsum('bhnck,bhnsk->bhncs', q_dec, k_inv)
    scores = jnp.where(causal_in_chunk, scores, 0.0)
    o_intra = jnp.einsum('bhncs,bhnsv->bhncv', scores, v)
    kv_chunk = jnp.einsum('bhnsk,bhnsv->bhnkv', k * jnp.exp(b_last - b), v)
    decay = jnp.exp(b_last[:, :, :, 0, :])

    def step(s, inp):
        d, u = inp
        return d[..., None] * s + u, s

    s_final, s_starts = lax.scan(step, s0, (jnp.moveaxis(decay, 2, 0), jnp.moveaxis(kv_chunk, 2, 0)))
    s_starts = jnp.moveaxis(s_starts, 0, 2)
    o_inter = jnp.einsum('bhnck,bhnkv->bhncv', q_dec, s_starts)
    return (o_intra + o_inter).reshape(B, H, T, DV), s_final


def reverse_gated_linear_scan(q, k, v, log_f, s0):
    flip = lambda a: jnp.flip(a, axis=2)
    o, s = chunked_gated_linear_scan(flip(q), flip(k), flip(v), flip(log_f), s0)
    return flip(o), s


def hgrn2_mixer(p_lat, p_ctx, lb, norm_g, need_ctx_out):
    splits = [HG_HEADS * HG_DK, 2 * HG_HEADS * HG_DK, 3 * HG_HEADS * HG_DK,
              3 * HG_HEADS * HG_DK + HG_HEADS * HG_DV]

    def prep(p):
        B, T, _ = p.shape
        heads = lambda a: a.reshape(B, T, HG_HEADS, -1).transpose(0, 2, 1, 3).astype(jnp.float32)
        q, f_fwd, f_bwd, i, g = jnp.split(p, splits, axis=-1)
        q = jax.nn.silu(heads(q))
        gates = []
        for d, f_pre in enumerate((f_fwd, f_bwd)):
            lb_d = lb[d].reshape(HG_HEADS, 1, HG_DK)
            f = lb_d + (1.0 - lb_d) * jax.nn.sigmoid(heads(f_pre))
            gates.append((1.0 - f, jnp.log(f)))
        return q, heads(i), gates, g

    def readout(o, g):
        B, H, T, DV = o.shape
        o = rms_norm(o, norm_g).transpose(0, 2, 1, 3).reshape(B, T, H * DV)
        return (o * jax.nn.silu(g.astype(jnp.float32))).astype(g.dtype)

    qc, ic, (gc_f, gc_b), g_c = prep(p_ctx)
    ql, il, (gl_f, gl_b), g_l = prep(p_lat)
    B = p_lat.shape[0]
    zero = jnp.zeros((B, HG_HEADS, HG_DK, HG_DV), jnp.float32)
    oc_f, s_ctx_f = chunked_gated_linear_scan(qc, gc_f[0], ic, gc_f[1], zero)
    oc_b, s_ctx_b = reverse_gated_linear_scan(qc, gc_b[0], ic, gc_b[1], zero)
    ol_f, _ = chunked_gated_linear_scan(ql, gl_f[0], il, gl_f[1], s_ctx_f)
    ol_b, _ = reverse_gated_linear_scan(ql, gl_b[0], il, gl_b[1], s_ctx_b)
    lat_out = readout(ol_f + ol_b, g_l)
    ctx_out = readout(oc_f + oc_b, g_c) if need_ctx_out else None
    return lat_out, ctx_out


def window_attention(p_lat, p_ctx, q_g, k_g, sink, row, col, need_ctx_out):
    G = ATT_HEADS // ATT_KV_HEADS
    scale = HEAD_DIM ** -0.5

    def split_heads(p):
        B, T, _ = p.shape
        q, k, v = jnp.split(p, [ATT_HEADS * HEAD_DIM, (ATT_HEADS + ATT_KV_HEADS) * HEAD_DIM], axis=-1)
        q = rms_norm(q.reshape(B, T, ATT_KV_HEADS, G, HEAD_DIM), q_g)
        k = rms_norm(k.reshape(B, T, ATT_KV_HEADS, HEAD_DIM), k_g)
        return q, k, v.reshape(B, T, ATT_KV_HEADS, HEAD_DIM)

    ql, kl, vl = split_heads(p_lat)
    qc, kc, vc = split_heads(p_ctx)
    ql = axial_rope(ql, row, col)
    kl = axial_rope(kl, row, col)
    B, T = p_lat.shape[:2]
    L = p_ctx.shape[1]
    NB = T // ATT_BLOCK
    sink_f = sink.astype(jnp.float32).reshape(ATT_KV_HEADS, G, 1, 1)

    qb = ql.reshape(B, NB, ATT_BLOCK, ATT_KV_HEADS, G, HEAD_DIM)
    pad = ((0, 0), (ATT_BLOCK, ATT_BLOCK), (0, 0), (0, 0))
    kp = jnp.pad(kl, pad).reshape(B, NB + 2, ATT_BLOCK, ATT_KV_HEADS, HEAD_DIM)
    vp = jnp.pad(vl, pad).reshape(B, NB + 2, ATT_BLOCK, ATT_KV_HEADS, HEAD_DIM)
    k_band = jnp.concatenate([kp[:, :-2], kp[:, 1:-1], kp[:, 2:]], axis=2)
    v_band = jnp.concatenate([vp[:, :-2], vp[:, 1:-1], vp[:, 2:]], axis=2)
    s_band = jnp.einsum('bnqhgd,bnkhd->bhgnqk', qb, k_band).astype(jnp.float32) * scale
    s_ctx = jnp.einsum('bnqhgd,bkhd->bhgnqk', qb, kc).astype(jnp.float32) * scale
    qi = jnp.arange(ATT_BLOCK)[:, None]
    kj = jnp.arange(3 * ATT_BLOCK)[None, :]
    within = jnp.abs(kj - ATT_BLOCK - qi) <= WINDOW
    k_pos = (jnp.arange(NB)[:, None] - 1) * ATT_BLOCK + kj
    in_range = (k_pos >= 0) & (k_pos < T)
    mask = within[None, :, :] & in_range[:, None, :]
    s_band = jnp.where(mask, s_band, NEG_INF)
    sink_col = jnp.broadcast_to(sink_f[None, :, :, :, :, None], s_band.shape[:-1] + (1,))
    probs = jax.nn.softmax(jnp.concatenate([s_band, s_ctx, sink_col], axis=-1), axis=-1)
    p_band = probs[..., :3 * ATT_BLOCK].astype(vl.dtype)
    p_ctxk = probs[..., 3 * ATT_BLOCK:3 * ATT_BLOCK + L].astype(vc.dtype)
    o_lat = (jnp.einsum('bhgnqk,bnkhd->bnqhgd', p_band, v_band)
             + jnp.einsum('bhgnqk,bkhd->bnqhgd', p_ctxk, vc))
    lat_out = o_lat.reshape(B, T, ATT_WIDTH)

    ctx_out = None
    if need_ctx_out:
        sc = jnp.einsum('bqhgd,bkhd->bhgqk', qc, kc).astype(jnp.float32) * scale
        sink_c = jnp.broadcast_to(sink_f[None, :, :, :, :], sc.shape[:-1] + (1,))
        pc = jax.nn.softmax(jnp.concatenate([sc, sink_c], axis=-1), axis=-1)[..., :L]
        o_c = jnp.einsum('bhgqk,bkhd->bqhgd', pc.astype(vc.dtype), vc)
        ctx_out = o_c.reshape(B, L, ATT_WIDTH)
    return lat_out, ctx_out


def swiglu(h, w_gate_up, w_down):
    gate, up = jnp.split(h @ w_gate_up, 2, axis=-1)
    return (jax.nn.silu(gate) * up) @ w_down


def setup_inputs(seed: int = 0) -> dict:
    key = jax.random.key(seed)
    ks = jax.random.split(key, 17)
    nrm = lambda k, shape, s: s * jax.random.normal(k, shape, jnp.float32)
    hg_lb = nrm(ks[9], (DEPTH + 1, 2, HG_HEADS * HG_DK), 0.1).at[DEPTH].add(LB_SLACK)
    return {
        "x": nrm(ks[0], (BATCH, SEQ, D_MODEL), 1.0),
        "c": nrm(ks[1], (BATCH, D_MODEL), 1.0),
        "ctx": nrm(ks[2], (BATCH, CTX_LEN, D_MODEL), 1.0),
        "c_ctx": nrm(ks[3], (D_MODEL,), 1.0),
        "w_mod": nrm(ks[4], (DEPTH, D_MODEL, N_MOD * D_MODEL), 0.5 * D_MODEL ** -0.5),
        "b_mod": nrm(ks[5], (DEPTH, N_MOD * D_MODEL), 0.01),
        "norm_mix_g": 1.0 + nrm(ks[6], (DEPTH, D_MODEL), 0.05),
        "norm_ffn_g": 1.0 + nrm(ks[7], (DEPTH, D_MODEL), 0.05),
        "w_in": nrm(ks[8], (DEPTH, D_MODEL, IN_COLS), D_MODEL ** -0.5),
        "hg_lb": hg_lb,
        "hg_norm_g": 1.0 + nrm(ks[10], (DEPTH, HG_DV), 0.05),
        "q_norm_g": 1.0 + nrm(ks[11], (DEPTH, HEAD_DIM), 0.05),
        "k_norm_g": 1.0 + nrm(ks[12], (DEPTH, HEAD_DIM), 0.05),
        "attn_sink": nrm(ks[13], (DEPTH, ATT_HEADS), 1.0),
        "w_out": nrm(ks[14], (DEPTH, D_MIX, D_MODEL), D_MIX ** -0.5),
        "w_gate_up": nrm(ks[15], (DEPTH, D_MODEL, 2 * D_FF), D_MODEL ** -0.5),
        "w_down": nrm(ks[16], (DEPTH, D_FF, D_MODEL), D_FF ** -0.5),
    }


def reference(x, c, ctx, c_ctx, w_mod, b_mod, norm_mix_g, norm_ffn_g, w_in, hg_lb,
              hg_norm_g, q_norm_g, k_norm_g, attn_sink, w_out, w_gate_up, w_down):
    T = x.shape[1]
    rows = T // GRID_W
    row = jnp.broadcast_to(jnp.arange(rows)[:, None], (rows, GRID_W)).reshape(-1)
    col = jnp.broadcast_to(jnp.arange(GRID_W)[None, :], (rows, GRID_W)).reshape(-1)
    lb_all = jnp.cumsum(jax.nn.softmax(hg_lb.astype(jnp.float32), axis=0), axis=0)
    y = ctx
    for l in range(DEPTH):
        need_ctx_out = l < DEPTH - 1
        sh_m, sc_m, gt_m, sh_f, sc_f, gt_f = [m[:, None, :] for m in
            jnp.split(jax.nn.silu(c) @ w_mod[l] + b_mod[l], N_MOD, axis=-1)]
        csh_m, csc_m, cgt_m, csh_f, csc_f, cgt_f = jnp.split(
            jax.nn.silu(c_ctx) @ w_mod[l] + b_mod[l], N_MOD, axis=-1)

        h_lat = modulate(rms_norm(x, norm_mix_g[l]), sh_m, sc_m)
        h_ctx = modulate(rms_norm(y, norm_mix_g[l]), csh_m, csc_m)
        p_lat = h_lat @ w_in[l]
        p_ctx = h_ctx @ w_in[l]
        hg_lat, hg_ctx = hgrn2_mixer(p_lat[..., :HG_COLS], p_ctx[..., :HG_COLS],
                                     lb_all[l], hg_norm_g[l], need_ctx_out)
        at_lat, at_ctx = window_attention(p_lat[..., HG_COLS:], p_ctx[..., HG_COLS:],
                                          q_norm_g[l], k_norm_g[l], attn_sink[l],
                                          row, col, need_ctx_out)
        x = x + gt_m * (jnp.concatenate([hg_lat, at_lat], axis=-1) @ w_out[l])

        x = x + gt_f * swiglu(modulate(rms_norm(x, norm_ffn_g[l]), sh_f, sc_f),
                              w_gate_up[l], w_down[l])

        if need_ctx_out:
            y = y + cgt_m * (jnp.concatenate([hg_ctx, at_ctx], axis=-1) @ w_out[l])
            y = y + cgt_f * swiglu(modulate(rms_norm(y, norm_ffn_g[l]), csh_f, csc_f),
                                   w_gate_up[l], w_down[l])
    return x
```

```python
import numpy as np
from contextlib import ExitStack
import concourse.bass as bass
import concourse.mybir as mybir
from concourse.bass_utils import run_bass_kernel_spmd

F32 = mybir.dt.float32
BF16 = mybir.dt.bfloat16
AF = mybir.ActivationFunctionType
ALU = mybir.AluOpType
AX = mybir.AxisListType

D = 2048
KT = 16
T_OWN = 1024
NT = 8
DFF = 5632
NFB = 44
EPS = 1e-6
SCALE = 128.0 ** -0.5
C_ATQ, C_ATKV, C_Q, C_I, C_FX, C_FY, C_G, C_FS1 = 0, 1024, 1536, 2560, 3584, 4608, 5632, 6656
W_IN_COLS = 7680


class Buf:
    __slots__ = ("name", "w", "r")

    def __init__(self, name):
        self.name = name
        self.w = None
        self.r = {}


class Sched:
    def __init__(self, nc, es):
        self.nc = nc
        self.es = es
        self.engs = ["pe", "act", "dve", "pool", "sp"]
        self.streams = {e: [] for e in self.engs}
        self.sem = {e: es.enter_context(nc.semaphore("s_" + e)) for e in ["pe", "act", "dve", "pool"]}
        self.cnt = {e: 0 for e in ["pe", "act", "dve", "pool"]}
        self.seen = {e: {} for e in self.engs}
        self.chan = {}
        self.out_ch = []

    def _waits(self, eng, reads, writes):
        waits = {}

        def need(t):
            if t is None:
                return
            sem, val, teng = t
            if teng == eng:
                return
            k = id(sem)
            if self.seen[eng].get(k, 0) >= val:
                return
            if k not in waits or waits[k][1] < val:
                waits[k] = (sem, val)

        for b in reads:
            need(b.w)
        for b in writes:
            need(b.w)
            for t in b.r.values():
                need(t)
        for k, (sem, val) in waits.items():
            self.seen[eng][k] = val
        return list(waits.values())

    def _commit(self, t, reads, writes):
        for b in reads:
            b.r[id(t[0])] = t
        for b in writes:
            b.w = t
            b.r = {}

    def op(self, eng, fn, reads=(), writes=()):
        w = self._waits(eng, reads, writes)
        self.cnt[eng] += 1
        t = (self.sem[eng], self.cnt[eng], eng)
        self._commit(t, reads, writes)
        self.streams[eng].append((w, fn, (self.sem[eng], 1)))

    def dma(self, q, ch, out, in_, reads=(), writes=(), is_out=False):
        if ch not in self.chan:
            self.chan[ch] = [self.es.enter_context(self.nc.semaphore("d_" + ch)), 0]
            if is_out:
                self.out_ch.append(ch)
        c = self.chan[ch]
        w = self._waits(q, reads, writes)
        c[1] += 16
        t = (c[0], c[1], None)
        self._commit(t, reads, writes)
        self.streams[q].append((w, lambda e: e.dma_start(out=out, in_=in_), (c[0], 16)))

    def barrier(self, bufs=()):
        for e in self.engs:
            w = []
            for e2 in ["pe", "act", "dve", "pool"]:
                if e2 != e and self.cnt[e2] > self.seen[e].get(id(self.sem[e2]), 0):
                    w.append((self.sem[e2], self.cnt[e2]))
                    self.seen[e][id(self.sem[e2])] = self.cnt[e2]
            for ch, (s, v) in self.chan.items():
                if v > self.seen[e].get(id(s), 0):
                    w.append((s, v))
                    self.seen[e][id(s)] = v
            if w:
                self.streams[e].append((w, None, None))

    def emit(self, block):
        nc = self.nc
        m = {"pe": block.tensor, "act": block.scalar, "dve": block.vector, "pool": block.gpsimd, "sp": block.sync}
        for e in self.engs:
            stream = self.streams[e]
            if not stream:
                continue

            def body(eng, stream=stream):
                for waits, fn, inc in stream:
                    for sem, val in waits:
                        eng.wait_ge(sem, val)
                    if fn is not None:
                        ins = fn(eng)
                        ins.then_inc(inc[0], inc[1])

            m[e](body)


import os


class _Stop(Exception):
    pass


def _stop_at(tag):
    if os.environ.get("MK_STOP", "") == tag:
        raise _Stop()


class Reg:
    def __init__(self, start):
        self.o = start

    def take(self, nbytes):
        o = self.o
        self.o += (nbytes + 63) // 64 * 64
        return o


def build_program():
    nc = bass.Bass("TRN2", target_bir_lowering=False)
    dt_in = lambda name, shape: nc.dram_tensor(name, list(shape), F32, kind="ExternalInput").ap()
    xo = dt_in("xo", [T_OWN, D])
    xs = dt_in("xs", [3 * T_OWN, D])
    xhc = dt_in("xhc", [512, D])
    cT = dt_in("cT", [128, 32])
    gvec = dt_in("gvec", [128, 32])
    w_mod = dt_in("w_mod", [D, 6 * D])
    b_mod = dt_in("b_mod", [1, 6 * D])
    w_in = dt_in("w_in_r", [D, W_IN_COLS])
    lbp = dt_in("lbp", [3, 2 * 1024])
    ngv = dt_in("ngv", [1, 3 * 128])
    sinkv = dt_in("sinkv", [1, 8])
    w_out = dt_in("w_out", [D, D])
    w_gu = dt_in("w_gu", [D, 2 * DFF])
    w_dn = dt_in("w_dn", [DFF, D])
    cmat = dt_in("cmat", [8, 128, 128])
    amask = dt_in("amask", [3, 128, 512])
    rope = dt_in("rope", [1280, 256])
    flg = dt_in("flg", [128, 2])
    y = nc.dram_tensor("y", [T_OWN, D], F32, kind="ExternalOutput").ap()

    es = ExitStack()
    with es:
        S = Sched(nc, es)
        LIMIT = 229344

        def T(name, shape, dtype, reg):
            nbytes = int(np.prod(shape[1:])) * (4 if dtype == F32 else 2)
            o = reg.take(nbytes)
            assert o + nbytes <= LIMIT, (name, o, nbytes)
            return nc.alloc_sbuf_tensor_at(name, list(shape), dtype, offset=o)

        banks = [es.enter_context(nc.psum_tensor("bank%d" % i, [128, 512], F32)) for i in range(8)]
        bbuf = [Buf("bank%d" % i) for i in range(8)]
        rot = {"big": [0, [0, 1, 2, 3]], "sm": [0, [4, 5]], "aux": [0, [6, 7]]}

        def ps(kind):
            r = rot[kind]
            i = r[1][r[0] % len(r[1])]
            r[0] += 1
            return banks[i], bbuf[i]

        try:
            P = Reg(16512)
            ident_f = T("ident_f", [128, 128], F32, P)
            ident_b = T("ident_b", [128, 128], BF16, P)
            ones_b = T("ones_b", [128, 128], BF16, P)
            ones_f = T("ones_f", [128, 128], F32, P)
            cm = T("cm", [128, 8, 128], F32, P)
            am = T("am", [128, 3, 512], BF16, P)
            flg_t = T("flg_t", [128, 2], F32, P)
            cT_t = T("cT_t", [128, 32], F32, P)
            scT = T("scT", [128, 32], BF16, P)
            gv = T("gv", [128, 32], F32, P)
            modT = T("modT", [128, 96, 2], F32, P)
            coefA = T("coefA", [128, 3, 16], F32, P)
            oml_t = T("oml_t", [128, 3, 1024], F32, P)
            ng_t = T("ng_t", [128, 3, 128], F32, P)
            es_t = T("es_t", [128, 8], F32, P)
            stX = T("stX", [128, 1024], F32, P)
            stY = T("stY", [128, 1024], F32, P)
            sbf = T("sbf", [128, 1024], BF16, P)
            R0 = P.o
            Bc = Buf("const")
            B_st = {k: Buf("st" + k) for k in ["X", "Y", "W", "T", "bf"]}
            B_modT, B_coef = Buf("modT"), Buf("coef")

            S.op("pool", lambda e: e.memset(ones_f[:], 1.0), writes=[Bc])
            S.op("pool", lambda e: e.memset(ones_b[:], 1.0), writes=[Bc])
            S.dma("sp", "c0", cm[:], cmat.rearrange("k p n -> p k n"), writes=[Bc])
            S.dma("sp", "c0", ident_f[:], cmat[6], writes=[Bc])
            S.dma("sp", "c0", flg_t[:], flg, writes=[Bc])
            S.dma("sp", "c0", cT_t[:], cT, writes=[Bc])
            S.dma("sp", "c0", gv[:], gvec, writes=[Bc])
            S.dma("sp", "c0", ng_t[:], ngv.partition_broadcast(128).rearrange("p o (k n) -> p (o k) n", k=3), writes=[Bc])
            S.dma("sp", "c0", es_t[:], sinkv.partition_broadcast(128).rearrange("p o n -> p (o n)"), writes=[Bc])
            S.dma("pool", "c1", am[:], amask.rearrange("k p n -> p k n"), writes=[Bc])
            S.op("dve", lambda e: e.tensor_copy(out=ident_b[:], in_=ident_f[:]), reads=[Bc], writes=[Bc])
            S.op("act", lambda e: e.activation(out=scT[:], in_=cT_t[:], func=AF.Silu), reads=[Bc], writes=[Bc])
            S.op("act", lambda e: e.activation(out=es_t[:], in_=es_t[:], func=AF.Exp), reads=[Bc], writes=[Bc])

            def load_w(dst, dbuf, ch, src, c0, ncols, nk=KT, k0=0):
                S.dma("pool", ch, dst, src[k0 * 128:(k0 + nk) * 128, c0:c0 + ncols].rearrange("(kt p) n -> p kt n", p=128), writes=[dbuf])

            def rstd_ops(ap, buf, n_inv):
                S.op("dve", lambda e: e.tensor_scalar(out=ap, in0=ap, scalar1=n_inv, scalar2=EPS, op0=ALU.mult, op1=ALU.add), reads=[buf], writes=[buf])
                S.op("act", lambda e: e.activation(out=ap, in_=ap, func=AF.Sqrt), reads=[buf], writes=[buf])
                S.op("dve", lambda e: e.reciprocal(out=ap, in_=ap), reads=[buf], writes=[buf])

            class NormCtx:
                def __init__(self, reg, tag, ns=2):
                    self.ns = ns
                    self.xt = [T("xt%s%d" % (tag, i), [128, D], F32, reg) for i in range(ns)]
                    self.xn = [T("xn%s%d" % (tag, i), [128, D], BF16, reg) for i in range(ns)]
                    self.scr = T("scr" + tag, [128, D], BF16, reg)
                    self.ssq = T("ssq" + tag, [128, 4], F32, reg)
                    self.Bxt = [Buf("xt0"), Buf("xt1")]
                    self.Bxn = [Buf("xn0"), Buf("xn1")]
                    self.Bscr = Buf("scr")
                    self.Bssq = [Buf("ssq0"), Buf("ssq1")]
                    self.n = 0
                    self.tag = tag

                def from_sbuf(self, xt, xtbuf, hT_dst, hbuf, tok0, which):
                    i = self.n % self.ns
                    self.n += 1
                    r = {"lat": 0, "ctx": 1, "ffn": 2}[which]
                    if which == "ffn":
                        Bcol = lambda j: modT[:, 48 + j, 0:1]
                    else:
                        Bcol = lambda j: modT[:, j, r:r + 1]
                    Acol = lambda j: coefA[:, r, j:j + 1]
                    ssq = self.ssq[:, i:i + 1]
                    xn = self.xn[i]
                    S.op("act", lambda e: e.activation(out=self.scr[:], in_=xt, func=AF.Square, accum_out=ssq), reads=[xtbuf], writes=[self.Bscr, self.Bssq[i]])
                    rstd_ops(ssq, self.Bssq[i], 1.0 / D)
                    S.op("dve", lambda e: e.tensor_scalar(out=xn[:], in0=xt, scalar1=ssq, scalar2=None, op0=ALU.mult), reads=[xtbuf, self.Bssq[i]], writes=[self.Bxn[i]])
                    for half in range(2):
                        pb, pbuf = ps("sm")
                        pv = pb[:, :].bitcast(BF16)

                        def f(e, half=half, pv=pv):
                            ins = None
                            for jj in range(8):
                                j = half * 8 + jj
                                ins = e.transpose(out=pv[:, jj * 128:(jj + 1) * 128], in_=xn[:, j * 128:(j + 1) * 128], identity=ident_b[:])
                            return ins
                        S.op("pe", f, reads=[self.Bxn[i], Bc], writes=[pbuf])
                        for jj in range(8):
                            j = half * 8 + jj
                            if jj % 2 == 0:
                                S.op("act", lambda e, j=j, jj=jj, pv=pv: e.activation(out=hT_dst[:, j, tok0:tok0 + 128], in_=pv[:, jj * 128:(jj + 1) * 128], func=AF.Identity, scale=Acol(j), bias=Bcol(j)), reads=[pbuf, B_coef, B_modT], writes=[hbuf])
                            else:
                                S.op("dve", lambda e, j=j, jj=jj, pv=pv: e.tensor_scalar(out=hT_dst[:, j, tok0:tok0 + 128], in0=pv[:, jj * 128:(jj + 1) * 128], scalar1=Acol(j), scalar2=Bcol(j), op0=ALU.mult, op1=ALU.add), reads=[pbuf, B_coef, B_modT], writes=[hbuf])

                def from_dram(self, rows, hT_dst, hbuf, tok0, which):
                    i = self.n % self.ns
                    S.dma("sp", "xt%s%d" % (self.tag, i), self.xt[i][:], rows, writes=[self.Bxt[i]])
                    self.from_sbuf(self.xt[i][:], self.Bxt[i], hT_dst, hbuf, tok0, which)

            A = Reg(R0)
            wA_i = T("wA_i", [128, KT, 1024], BF16, A)
            wA_f1 = T("wA_f1", [128, KT, 1024], BF16, A)
            wA_fY = T("wA_fY", [128, KT, 1024], BF16, A)
            B_wAi, B_wAf1, B_wAfY = Buf("wAi"), Buf("wAf1"), Buf("wAfY")
            stW = T("stW", [128, 1024], F32, A)
            stT = T("stT", [128, 1024], F32, A)
            hT_hc = T("hT_hc", [128, KT, 512], BF16, A)
            B_hThc = Buf("hT_hc")
            HC_OFF = A.o - 16384
            nA = NormCtx(A, "A", 2)
            hT_g = [T("hT_g%d" % i, [128, KT, 128], BF16, A) for i in range(2)]
            B_hTg = [Buf("hTg0"), Buf("hTg1")]
            lf = T("lf", [128, 1024], F32, A)
            kk = T("kk", [128, 1024], F32, A)
            lfx = T("lfx", [128, 1024], F32, A)
            kd = T("kd", [128, 1024], BF16, A)
            vv = T("vv", [128, 1024], BF16, A)
            dec = T("dec", [128, 8], F32, A)
            B_lf, B_kk, B_lfx, B_kd, B_vv, B_dec = [Buf(n) for n in ["lf", "kk", "lfx", "kd", "vv", "dec"]]
            assert A.o <= LIMIT, A.o
            M = Reg(R0)
            wm = T("wm", [128, KT, 512], BF16, M)
            mrow = T("mrow", [2, 512], F32, M)
            bmt = T("bmt", [2, 512], F32, M)
            lbraw = T("lbraw", [128, 2, 1024], F32, M)
            B_wm, B_mrow, B_bmt, B_lbraw = Buf("wm"), Buf("mrow"), Buf("bmt"), Buf("lbraw")

            for k in range(3):
                S.dma("sp", "c2", lbraw[:], lbp[k:k + 1, :].partition_broadcast(128).rearrange("p o (d n) -> p (o d) n", d=2), writes=[B_lbraw])
                S.op("dve", lambda e, k=k: e.tensor_tensor(out=oml_t[:, k, :], in0=lbraw[:, 1, :], in1=lbraw[:, 0, :], op=ALU.subtract), reads=[B_lbraw], writes=[Bc])
                S.op("act", lambda e, k=k: e.activation(out=oml_t[:, k, :], in_=oml_t[:, k, :], func=AF.Sigmoid), reads=[Bc], writes=[Bc])

            def mod_block(blk):
                load_w(wm[:], B_wm, "wm", w_mod, blk * 512, 512)
                S.dma("sp", "bm", bmt[:], b_mod[:, blk * 512:(blk + 1) * 512].partition_broadcast(2).rearrange("p o n -> p (o n)"), writes=[B_bmt])
                pb, pbuf = ps("big")

                def f(e):
                    ins = None
                    for kt in range(KT):
                        ins = e.matmul(pb[0:2, :], lhsT=scT[:, 2 * kt:2 * kt + 2], rhs=wm[:, kt, :], start=(kt == 0), stop=(kt == KT - 1))
                    return ins
                S.op("pe", f, reads=[B_wm, Bc], writes=[pbuf])
                S.op("dve", lambda e: e.tensor_tensor(out=mrow[:], in0=pb[0:2, :], in1=bmt[:], op=ALU.add), reads=[pbuf, B_bmt], writes=[B_mrow])
                p2, p2buf = ps("sm")

                def g(e):
                    ins = None
                    for jj in range(4):
                        ins = e.matmul(p2[:, 2 * jj:2 * jj + 2], lhsT=mrow[0:2, jj * 128:(jj + 1) * 128], rhs=ident_f[0:2, 0:2], start=True, stop=True)
                    return ins
                S.op("pe", g, reads=[B_mrow, Bc], writes=[p2buf])
                S.op("act", lambda e: e.copy(out=modT[:, blk * 4:(blk + 1) * 4, :], in_=p2[:, 0:8].rearrange("p (j r) -> p j r", r=2)), reads=[p2buf], writes=[B_modT])

            for blk in range(24):
                mod_block(blk)
            for r in range(2):
                S.op("dve", lambda e, r=r: e.scalar_tensor_tensor(out=coefA[:, r, :], in0=modT[:, 16:32, r], scalar=1.0, in1=gv[:, 0:16], op0=ALU.add, op1=ALU.mult), reads=[B_modT, Bc], writes=[B_coef])
            S.op("dve", lambda e: e.scalar_tensor_tensor(out=coefA[:, 2, :], in0=modT[:, 64:80, 0], scalar=1.0, in1=gv[:, 16:32], op0=ALU.add, op1=ALU.mult), reads=[B_modT, Bc], writes=[B_coef])

            _stop_at("S1")
            S.barrier()
            for t in range(4):
                nA.from_dram(xhc[t * 128:(t + 1) * 128, :], hT_hc, B_hThc, t * 128, "lat" if t < 2 else "ctx")

            for hb in range(2):
                load_w(wA_i[:, :, hb * 512:(hb + 1) * 512], B_wAi, "wAi", w_in, C_I + hb * 512, 512)
                load_w(wA_f1[:, :, hb * 512:(hb + 1) * 512], B_wAf1, "wAf1", w_in, C_FX + hb * 512, 512)
                load_w(wA_fY[:, :, hb * 512:(hb + 1) * 512], B_wAfY, "wAfY", w_in, C_FY + hb * 512, 512)

            def proj_tok(hT_src, hbuf, tok0, wt, wbuf, c0):
                pb, pbuf = ps("big")

                def f(e):
                    ins = None
                    for kt in range(KT):
                        ins = e.matmul(pb[:, :], lhsT=hT_src[:, kt, tok0:tok0 + 128], rhs=wt[:, kt, c0:c0 + 512], start=(kt == 0), stop=(kt == KT - 1))
                    return ins
                S.op("pe", f, reads=[hbuf, wbuf], writes=[pbuf])
                return pb, pbuf

            def gate_prep(pb, pbuf, lbi, c0, lf_dst, kk_dst, lfbuf, kkbuf):
                S.op("act", lambda e: e.activation(out=kk_dst, in_=pb[:, :], func=AF.Sigmoid, scale=-1.0), reads=[pbuf], writes=[kkbuf])
                S.op("dve", lambda e: e.tensor_tensor(out=kk_dst, in0=kk_dst, in1=oml_t[:, lbi, c0:c0 + 512], op=ALU.mult), reads=[Bc], writes=[kkbuf])
                S.op("act", lambda e: e.activation(out=lf_dst, in_=kk_dst, func=AF.Ln, scale=-1.0, bias=1.0), reads=[kkbuf], writes=[lfbuf])

            def state_tile(hT_src, hbuf, tok0, wf, wfbuf, lbi, m2_idx, stk, st):
                for hb in range(2):
                    pb, pbuf = proj_tok(hT_src, hbuf, tok0, wf, wfbuf, hb * 512)
                    gate_prep(pb, pbuf, lbi, hb * 512, lf[:, hb * 512:(hb + 1) * 512], kk[:, hb * 512:(hb + 1) * 512], B_lf, B_kk)
                    pb, pbuf = proj_tok(hT_src, hbuf, tok0, wA_i, B_wAi, hb * 512)
                    S.op("act", lambda e, pb=pb, hb=hb: e.copy(out=vv[:, hb * 512:(hb + 1) * 512], in_=pb[:, :]), reads=[pbuf], writes=[B_vv])
                for hb in range(2):
                    pb, pbuf = ps("big")
                    S.op("pe", lambda e, pb=pb, hb=hb: e.matmul(pb[:, :], lhsT=cm[:, m2_idx, :], rhs=lf[:, hb * 512:(hb + 1) * 512], start=True, stop=True), reads=[B_lf, Bc], writes=[pbuf])
                    S.op("act", lambda e, pb=pb, hb=hb: e.activation(out=lfx[:, hb * 512:(hb + 1) * 512], in_=pb[:, :], func=AF.Exp), reads=[pbuf], writes=[B_lfx])
                    S.op("dve", lambda e, hb=hb: e.tensor_tensor(out=kd[:, hb * 512:(hb + 1) * 512], in0=kk[:, hb * 512:(hb + 1) * 512], in1=lfx[:, hb * 512:(hb + 1) * 512], op=ALU.mult), reads=[B_kk, B_lfx], writes=[B_kd])
                p2, p2buf = ps("sm")

                def f(e):
                    ins = None
                    for h in range(8):
                        ins = e.matmul(p2[:, 2 * h:2 * h + 2], lhsT=lf[:, h * 128:(h + 1) * 128], rhs=ones_f[:, 0:2], start=True, stop=True)
                    return ins
                S.op("pe", f, reads=[B_lf, Bc], writes=[p2buf])
                S.op("act", lambda e: e.activation(out=dec[:], in_=p2[:, 0:16].rearrange("p (h r) -> p h r", r=2)[:, :, 0], func=AF.Exp), reads=[p2buf], writes=[B_dec])
                for hb in range(2):
                    pb, pbuf = ps("aux")

                    def g(e, pb=pb, hb=hb):
                        ins = None
                        for hh in range(4):
                            h = hb * 4 + hh
                            ins = e.matmul(pb[:, hh * 128:(hh + 1) * 128], lhsT=kd[:, h * 128:(h + 1) * 128], rhs=vv[:, h * 128:(h + 1) * 128], start=True, stop=True)
                        return ins
                    S.op("pe", g, reads=[B_kd, B_vv], writes=[pbuf])
                    for hh in range(4):
                        h = hb * 4 + hh
                        S.op("dve", lambda e, pb=pb, h=h, hh=hh: e.scalar_tensor_tensor(out=st[:, h * 128:(h + 1) * 128], in0=st[:, h * 128:(h + 1) * 128], scalar=dec[:, h:h + 1], in1=pb[:, hh * 128:(hh + 1) * 128], op0=ALU.mult, op1=ALU.add), reads=[pbuf, B_dec], writes=[B_st[stk]])

            S.op("pool", lambda e: e.memset(stX[:], 0.0), writes=[B_st["X"]])
            S.op("pool", lambda e: e.memset(stY[:], 0.0), writes=[B_st["Y"]])
            for t in (0, 1):
                state_tile(hT_hc, B_hThc, 256 + t * 128, wA_f1, B_wAf1, 0, 2, "X", stX)
            for t in (1, 0):
                state_tile(hT_hc, B_hThc, 256 + t * 128, wA_fY, B_wAfY, 1, 3, "Y", stY)
            _stop_at("S2")
            for hb in range(2):
                load_w(wA_f1[:, :, hb * 512:(hb + 1) * 512], B_wAf1, "wAf1", w_in, C_FS1 + hb * 512, 512)
            mcol = flg_t[:, 0:1]
            omcol = flg_t[:, 1:2]
            S.op("dve", lambda e: e.tensor_scalar(out=stT[:], in0=stX[:], scalar1=omcol, scalar2=None, op0=ALU.mult), reads=[B_st["X"], Bc], writes=[B_st["T"]])
            S.op("dve", lambda e: e.scalar_tensor_tensor(out=stW[:], in0=stY[:], scalar=mcol, in1=stT[:], op0=ALU.mult, op1=ALU.add), reads=[B_st["Y"], B_st["T"]], writes=[B_st["W"]])

            def slot(si, wf, wfbuf, lbi):
                for t in range(8):
                    gi = t % 2
                    nA.from_dram(xs[si * 1024 + t * 128: si * 1024 + (t + 1) * 128, :], hT_g[gi], B_hTg[gi], 0, "lat")
                    state_tile(hT_g[gi], B_hTg[gi], 0, wf, wfbuf, lbi, 2, "W", stW)

            slot(0, wA_f1, B_wAf1, 2)
            S.op("dve", lambda e: e.tensor_scalar(out=stT[:], in0=stW[:], scalar1=omcol, scalar2=None, op0=ALU.mult), reads=[B_st["W"]], writes=[B_st["T"]])
            S.op("dve", lambda e: e.scalar_tensor_tensor(out=stX[:], in0=stX[:], scalar=mcol, in1=stT[:], op0=ALU.mult, op1=ALU.add), reads=[B_st["T"]], writes=[B_st["X"]])
            S.op("dve", lambda e: e.tensor_scalar(out=stT[:], in0=stY[:], scalar1=omcol, scalar2=None, op0=ALU.mult), reads=[B_st["Y"]], writes=[B_st["T"]])
            S.op("dve", lambda e: e.scalar_tensor_tensor(out=stW[:], in0=stW[:], scalar=mcol, in1=stT[:], op0=ALU.mult, op1=ALU.add), reads=[B_st["T"]], writes=[B_st["W"]])
            slot(1, wA_fY, B_wAfY, 1)
            slot(2, wA_fY, B_wAfY, 1)
            S.op("dve", lambda e: e.tensor_copy(out=stY[:], in_=stW[:]), reads=[B_st["W"]], writes=[B_st["Y"]])
            S.barrier()

            _stop_at("A")
            assert HC_OFF == R0 + 3 * 32768 + 8192
            Bm = Reg(R0)
            hT = T("hT", [128, KT, T_OWN], BF16, Bm)
            ws = T("ws", [128, KT, 512], BF16, Bm)
            qT = T("qT", [128, 8, T_OWN], BF16, Bm)
            mixB = T("mixB", [128, 8, T_OWN], BF16, Bm)
            vown_off = Bm.take(16384)
            gap = HC_OFF - Bm.o
            assert gap >= 0, gap
            Bm.take(gap)
            hc_keep = Bm.take(16384)
            oacc_off = Bm.take(32768)
            r14_off = Bm.take(14336)
            assert Bm.o <= LIMIT, Bm.o
            B_hT, B_qT, B_vown, B_oacc, B_mixA, B_mixB, B_ws, B_kv = [Buf(n) for n in ["hT", "qT", "vown", "oacc", "mixA", "mixB", "ws", "kv"]]
            v_own = T("v_own", [128, NT, 1024], BF16, Reg(vown_off))
            o_acc = T("o_acc", [128, NT, 1024], F32, Reg(oacc_off))
            mixA = T("mixA", [128, 8, T_OWN], BF16, Reg(hc_keep))
            nB = NormCtx(Reg(oacc_off), "B")
            for t in range(NT):
                nB.from_dram(xo[t * 128:(t + 1) * 128, :], hT, B_hT, t * 128, "lat")
            S.barrier()
            rr = Reg(vown_off)
            rp = T("rp", [128, 10, 256], F32, rr)
            gsw = T("gsw", [128, 2, 128], F32, rr)
            ro = Reg(oacc_off)
            tabq = T("tabq", [128, 8, 256], F32, ro)
            tabk = T("tabk", [128, 10, 256], F32, ro)
            a_x = T("a_x", [128, 512], F32, ro)
            a_t = T("a_t", [128, 512], F32, ro)
            a_u = T("a_u", [128, 512], F32, ro)
            a_o = T("a_o", [128, 512], BF16, ro)
            a_sc = T("a_sc", [128, 512], BF16, ro)
            a_ss = T("a_ss", [128, 8], F32, ro)
            assert ro.o <= oacc_off + 32768
            B_tab, B_rp, B_gsw = Buf("tab"), Buf("rp"), Buf("gsw")
            B_ax, B_at, B_au, B_ao, B_ass, B_asc = [Buf(n) for n in ["a_x", "a_t", "a_u", "a_o", "a_ss", "a_sc"]]
            S.dma("sp", "c3", rp[:], rope.rearrange("(t p) n -> p t n", p=128), writes=[B_rp])
            for gi in range(2):
                for blk in range(4):
                    sblk = blk ^ 1
                    S.op("dve", lambda e, gi=gi, blk=blk, sblk=sblk: e.tensor_copy(out=gsw[:, gi, blk * 32:(blk + 1) * 32], in_=ng_t[:, 1 + gi, sblk * 32:(sblk + 1) * 32]), reads=[Bc], writes=[B_gsw])
            for t in range(10):
                if 1 <= t <= 8:
                    S.op("dve", lambda e, t=t: e.tensor_tensor(out=tabq[:, t - 1, 0:128], in0=rp[:, t, 0:128], in1=ng_t[:, 1, :], op=ALU.mult), reads=[B_rp, Bc], writes=[B_tab])
                    S.op("dve", lambda e, t=t: e.tensor_tensor(out=tabq[:, t - 1, 128:256], in0=rp[:, t, 128:256], in1=gsw[:, 0, :], op=ALU.mult), reads=[B_rp, B_gsw], writes=[B_tab])
                S.op("dve", lambda e, t=t: e.tensor_tensor(out=tabk[:, t, 0:128], in0=rp[:, t, 0:128], in1=ng_t[:, 2, :], op=ALU.mult), reads=[B_rp, Bc], writes=[B_tab])
                S.op("dve", lambda e, t=t: e.tensor_tensor(out=tabk[:, t, 128:256], in0=rp[:, t, 128:256], in1=gsw[:, 1, :], op=ALU.mult), reads=[B_rp, B_gsw], writes=[B_tab])

            def next_w(src, c0, ncols=512, nk=KT, k0=0):
                load_w(ws[:, 0:nk, 0:ncols], B_ws, "ws", src, c0, ncols, nk, k0)
                return ws, B_ws

            def qk_norm_rope(pb, pbuf, nh, tab, ti, use_rope, gidx):
                for h in range(nh):
                    S.op("act", lambda e, h=h: e.activation(out=a_sc[:, h * 128:(h + 1) * 128], in_=pb[:, h * 128:(h + 1) * 128], func=AF.Square, accum_out=a_ss[:, h:h + 1]), reads=[pbuf], writes=[B_asc, B_ass])
                rstd_ops(a_ss[:, 0:nh], B_ass, 1.0 / 128)
                for h in range(nh):
                    hs = slice(h * 128, (h + 1) * 128)
                    S.op("act", lambda e, h=h, hs=hs: e.activation(out=a_x[:, hs], in_=pb[:, hs], func=AF.Copy, scale=a_ss[:, h:h + 1]), reads=[pbuf, B_ass], writes=[B_ax])
                    if use_rope:
                        S.op("dve", lambda e, hs=hs: e.tensor_tensor(out=a_t[:, hs], in0=a_x[:, hs], in1=tab[:, ti, 0:128], op=ALU.mult), reads=[B_ax, B_tab], writes=[B_at])
                        for blk in range(4):
                            sblk = blk ^ 1
                            S.op("pool", lambda e, h=h, blk=blk, sblk=sblk: e.tensor_tensor(out=a_u[:, h * 128 + blk * 32:h * 128 + (blk + 1) * 32], in0=a_x[:, h * 128 + sblk * 32:h * 128 + (sblk + 1) * 32], in1=tab[:, ti, 128 + blk * 32:128 + (blk + 1) * 32], op=ALU.mult), reads=[B_ax, B_tab], writes=[B_au])
                        S.op("dve", lambda e, hs=hs: e.tensor_tensor(out=a_o[:, hs], in0=a_t[:, hs], in1=a_u[:, hs], op=ALU.add), reads=[B_at, B_au], writes=[B_ao])
                    else:
                        S.op("dve", lambda e, hs=hs: e.tensor_tensor(out=a_o[:, hs], in0=a_x[:, hs], in1=ng_t[:, gidx, :], op=ALU.mult), reads=[B_ax, Bc], writes=[B_ao])

            def transpose_heads(nh, dst_fn, dbuf):
                p2, p2buf = ps("sm")
                pv = p2[:, :].bitcast(BF16)

                def f(e):
                    ins = None
                    for h in range(nh):
                        ins = e.transpose(out=pv[:, h * 128:(h + 1) * 128], in_=a_o[:, h * 128:(h + 1) * 128], identity=ident_b[:])
                    return ins
                S.op("pe", f, reads=[B_ao, Bc], writes=[p2buf])
                for h in range(nh):
                    S.op("act", lambda e, h=h: e.copy(out=dst_fn(h), in_=pv[:, h * 128:(h + 1) * 128]), reads=[p2buf], writes=[dbuf])

            rk = Reg(r14_off)
            kTb = T("kTb", [128, 2, 1280], BF16, rk)
            kTc = T("kTc", [128, 2, 256], BF16, rk)
            vb = T("vb", [128, 10, 256], BF16, rk)
            vc = T("vc", [128, 2, 256], BF16, rk)
            assert rk.o <= r14_off + 14336
            hT_hcB = T("hT_hcB", [128, KT, 512], BF16, Reg(hc_keep))
            wt, wb = next_w(w_in, C_ATKV)
            for ki in range(12):
                if ki == 0:
                    src, sbuf_, tok0 = hT_hcB, B_hThc, 0
                elif ki == 9:
                    src, sbuf_, tok0 = hT_hcB, B_hThc, 128
                elif ki < 9:
                    src, sbuf_, tok0 = hT, B_hT, (ki - 1) * 128
                else:
                    src, sbuf_, tok0 = hT_hcB, B_hThc, 256 + (ki - 10) * 128
                pb, pbuf = proj_tok(src, sbuf_, tok0, wt, wb, 0)
                is_ctx = ki >= 10
                qk_norm_rope(pb, pbuf, 2, tabk, ki if not is_ctx else 0, not is_ctx, 2)
                if is_ctx:
                    c = ki - 10
                    transpose_heads(2, lambda h, c=c: kTc[:, h, c * 128:(c + 1) * 128], B_kv)
                    S.op("act", lambda e, pb=pb, c=c: e.copy(out=vc[:, c, :], in_=pb[:, 256:512]), reads=[pbuf], writes=[B_kv])
                else:
                    transpose_heads(2, lambda h, k=ki: kTb[:, h, k * 128:(k + 1) * 128], B_kv)
                    S.op("act", lambda e, pb=pb, k=ki: e.copy(out=vb[:, k, :], in_=pb[:, 256:512]), reads=[pbuf], writes=[B_kv])
            for hb in range(2):
                wt, wb = next_w(w_in, C_ATQ + hb * 512)
                for t in range(NT):
                    pb, pbuf = proj_tok(hT, B_hT, t * 128, wt, wb, 0)
                    qk_norm_rope(pb, pbuf, 4, tabq, t, True, 1)
                    transpose_heads(4, lambda h, hb=hb, t=t: qT[:, hb * 4 + h, t * 128:(t + 1) * 128], B_qT)
            S.barrier()
            rp2 = Reg(vown_off)
            pT = [T("pT%d" % i, [128, 512], BF16, rp2) for i in range(5)]
            rden = T("rden", [128, 512], F32, rp2)
            B_pT = [Buf("pT%d" % i) for i in range(5)]
            B_rden = Buf("rden")
            for t in range(NT):
                for g in range(2):
                    keys = [(kTb[:, g, (t + d) * 128:(t + d + 1) * 128], vb[:, t + d, g * 128:(g + 1) * 128]) for d in range(3)]
                    keys += [(kTc[:, g, c * 128:(c + 1) * 128], vc[:, c, g * 128:(g + 1) * 128]) for c in range(2)]
                    for ki, (kap, vap) in enumerate(keys):
                        pb, pbuf = ps("big")
                        S.op("pe", lambda e, pb=pb, kap=kap, t=t, g=g: e.matmul(pb[:, :].rearrange("p (h q) -> p h q", h=4), lhsT=kap, rhs=qT[:, 4 * g:4 * g + 4, t * 128:(t + 1) * 128], start=True, stop=True), reads=[B_kv, B_qT], writes=[pbuf])
                        S.op("act", lambda e, pb=pb, ki=ki: e.activation(out=pT[ki][:], in_=pb[:, :], func=AF.Exp, scale=SCALE), reads=[pbuf], writes=[B_pT[ki]])
                        if ki == 0:
                            mi = 0 if t == 0 else 1
                            S.op("pool", lambda e, mi=mi: e.tensor_tensor(out=pT[0][:], in0=pT[0][:], in1=am[:, mi, :], op=ALU.mult), reads=[Bc], writes=[B_pT[0]])
                        if ki == 2:
                            S.op("pool", lambda e: e.tensor_tensor(out=pT[2][:], in0=pT[2][:], in1=am[:, 2, :], op=ALU.mult), reads=[Bc], writes=[B_pT[2]])
                    po, pobuf = ps("aux")
                    pd, pdbuf = ps("aux")

                    def f(e, po=po, keys=keys):
                        ins = None
                        for ki, (kap, vap) in enumerate(keys):
                            ins = e.matmul(po[:, :], lhsT=vap, rhs=pT[ki][:], start=(ki == 0), stop=(ki == 4))
                        return ins

                    def f2(e, pd=pd):
                        ins = None
                        for ki in range(5):
                            ins = e.matmul(pd[:, :], lhsT=ones_b[:], rhs=pT[ki][:], start=(ki == 0), stop=(ki == 4))
                        return ins
                    S.op("pe", f, reads=B_pT + [B_kv], writes=[pobuf])
                    S.op("pe", f2, reads=B_pT + [Bc], writes=[pdbuf])
                    for hh in range(4):
                        h = 4 * g + hh
                        S.op("dve", lambda e, pd=pd, h=h, hh=hh: e.tensor_scalar(out=rden[:, hh * 128:(hh + 1) * 128], in0=pd[:, hh * 128:(hh + 1) * 128], scalar1=es_t[:, h:h + 1], scalar2=None, op0=ALU.add), reads=[pdbuf, Bc], writes=[B_rden])
                    S.op("dve", lambda e: e.reciprocal(out=rden[:], in_=rden[:]), reads=[B_rden], writes=[B_rden])
                    S.op("dve", lambda e, po=po, t=t, g=g: e.tensor_tensor(out=mixB[:, 4 * g:4 * g + 4, t * 128:(t + 1) * 128], in0=po[:, :].rearrange("p (h q) -> p h q", h=4), in1=rden[:].rearrange("p (h q) -> p h q", h=4), op=ALU.mult), reads=[pobuf, B_rden], writes=[B_mixB])
            S.barrier()

            _stop_at("ATT")
            for hb in range(2):
                wt, wb = next_w(w_in, C_Q + hb * 512)
                for hh in range(4):
                    for half in range(2):
                        pb, pbuf = ps("big")

                        def f(e, pb=pb, wt=wt, hh=hh, half=half):
                            ins = None
                            for kt in range(KT):
                                ins = e.matmul(pb[:, :], lhsT=wt[:, kt, hh * 128:(hh + 1) * 128], rhs=hT[:, kt, half * 512:(half + 1) * 512], start=(kt == 0), stop=(kt == KT - 1))
                            return ins
                        S.op("pe", f, reads=[B_hT, wb], writes=[pbuf])
                        S.op("act", lambda e, pb=pb, hb=hb, hh=hh, half=half: e.activation(out=qT[:, hb * 4 + hh, half * 512:(half + 1) * 512], in_=pb[:, :], func=AF.Silu), reads=[pbuf], writes=[B_qT])
            for hb in range(2):
                wt, wb = next_w(w_in, C_I + hb * 512)
                for t in range(NT):
                    pb, pbuf = proj_tok(hT, B_hT, t * 128, wt, wb, 0)
                    S.op("act", lambda e, pb=pb, hb=hb, t=t: e.copy(out=v_own[:, t, hb * 512:(hb + 1) * 512], in_=pb[:, :]), reads=[pbuf], writes=[B_vown])

            rc = Reg(r14_off)
            h_lf = T("h_lf", [128, 512], F32, rc)
            h_k = T("h_k", [128, 512], F32, rc)
            h_kb = T("h_kb", [128, 512], BF16, rc)
            h_ex = T("h_ex", [128, 512], F32, rc)
            h_kd = T("h_kd", [128, 512], BF16, rc)
            h_E = T("h_E", [128, 128], F32, rc)
            h_Ei = T("h_Ei", [128, 128], F32, rc)
            h_kinv = T("h_kinv", [128, 128], BF16, rc)
            h_qdec = T("h_qdec", [128, 128], BF16, rc)
            h_ms = T("h_ms", [128, 128], BF16, rc)
            assert rc.o <= r14_off + 14336, rc.o - r14_off
            B_hlf, B_hk, B_hkb, B_hex, B_hkd, B_hE, B_hEi, B_hkinv, B_hqdec, B_hms = [Buf(n) for n in ["hlf", "hk", "hkb", "hex", "hkd", "hE", "hEi", "hkinv", "hqdec", "hms"]]

            hE4 = [h_E] + [T("h_E%d" % i, [128, 128], F32, rc) for i in range(1, 4)]
            hq4 = [h_qdec] + [T("h_qdec%d" % i, [128, 128], BF16, rc) for i in range(1, 4)]
            assert rc.o <= r14_off + 14336, rc.o - r14_off
            B_hE4 = [Buf("hE%d" % i) for i in range(4)]
            B_hq4 = [Buf("hq%d" % i) for i in range(4)]
            B_sth = {k: [Buf("st%s%d" % (k, h)) for h in range(8)] for k in ("X", "Y")}
            B_sbfh = [Buf("sbf%d" % h) for h in range(8)]

            def chain(which):
                X = which == "X"
                c0 = C_FX if X else C_FY
                lbi = 0 if X else 1
                tri = 0 if X else 1
                m2 = 4 if X else 5
                st = stX if X else stY
                stk = "X" if X else "Y"
                for hb in range(2):
                    wt, wb = next_w(w_in, c0 + hb * 512)
                    S.op("act", lambda e, hb=hb: e.copy(out=sbf[:, hb * 512:(hb + 1) * 512], in_=st[:, hb * 512:(hb + 1) * 512]), reads=[B_st[stk]] + B_sth[stk][hb * 4:hb * 4 + 4], writes=B_sbfh[hb * 4:hb * 4 + 4])
                    for t in (range(NT) if X else range(NT - 1, -1, -1)):
                        pb, pbuf = proj_tok(hT, B_hT, t * 128, wt, wb, 0)
                        gate_prep(pb, pbuf, lbi, hb * 512, h_lf[:], h_k[:], B_hlf, B_hk)
                        S.op("pool", lambda e: e.tensor_copy(out=h_kb[:], in_=h_k[:]), reads=[B_hk], writes=[B_hkb])
                        pd, pdbuf = ps("aux")
                        S.op("pe", lambda e, pd=pd: e.matmul(pd[:, :], lhsT=cm[:, m2, :], rhs=h_lf[:], start=True, stop=True), reads=[B_hlf, Bc], writes=[pdbuf])
                        S.op("act", lambda e, pd=pd: e.activation(out=h_ex[:], in_=pd[:, :], func=AF.Exp), reads=[pdbuf], writes=[B_hex])
                        S.op("dve", lambda e: e.tensor_tensor(out=h_kd[:], in0=h_k[:], in1=h_ex[:], op=ALU.mult), reads=[B_hk, B_hex], writes=[B_hkd])
                        pos = []
                        for hh in range(4):
                            h = hb * 4 + hh
                            hs = slice(hh * 128, (hh + 1) * 128)
                            E, Bq_E, qd, Bq_q = hE4[hh], B_hE4[hh], hq4[hh], B_hq4[hh]
                            p1, p1buf = ps("sm")
                            S.op("pe", lambda e, p1=p1, hs=hs: e.matmul(p1[:, 0:128], lhsT=h_lf[:, hs], rhs=cm[:, tri, :], start=True, stop=True), reads=[B_hlf, Bc], writes=[p1buf])
                            S.op("act", lambda e, p1=p1, E=E: e.activation(out=E[:], in_=p1[:, 0:128], func=AF.Exp), reads=[p1buf], writes=[Bq_E])
                            S.op("act", lambda e, p1=p1: e.activation(out=h_Ei[:], in_=p1[:, 0:128], func=AF.Exp, scale=-1.0), reads=[p1buf], writes=[B_hEi])
                            p2, p2buf = ps("sm")
                            p2v = p2[:, :].bitcast(BF16)
                            S.op("pe", lambda e, p2v=p2v, hs=hs: e.transpose(out=p2v[:, 0:128], in_=h_kb[:, hs], identity=ident_b[:]), reads=[B_hkb, Bc], writes=[p2buf])
                            S.op("dve", lambda e, p2v=p2v: e.tensor_tensor(out=h_kinv[:], in0=p2v[:, 0:128], in1=h_Ei[:], op=ALU.mult), reads=[p2buf, B_hEi], writes=[B_hkinv])
                            S.op("pool", lambda e, h=h, t=t, qd=qd, E=E: e.tensor_tensor(out=qd[:], in0=qT[:, h, t * 128:(t + 1) * 128], in1=E[:], op=ALU.mult), reads=[B_qT, Bq_E], writes=[Bq_q])
                            p3, p3buf = ps("sm")
                            S.op("pe", lambda e, p3=p3, qd=qd: e.matmul(p3[:, 0:128], lhsT=h_kinv[:], rhs=qd[:], start=True, stop=True), reads=[B_hkinv, Bq_q], writes=[p3buf])
                            S.op("dve", lambda e, p3=p3: e.tensor_tensor(out=h_ms[:], in0=p3[:, 0:128], in1=cm[:, tri, :], op=ALU.mult), reads=[p3buf, Bc], writes=[B_hms])
                            po, pobuf = banks[hh], bbuf[hh]
                            S.op("pe", lambda e, po=po, t=t, h=h: e.matmul(po[:, 0:128], lhsT=h_ms[:], rhs=v_own[:, t, h * 128:(h + 1) * 128], start=True, stop=False), reads=[B_hms, B_vown], writes=[pobuf])
                            pos.append((po, pobuf))
                        for j in (range(4) if X else range(3, -1, -1)):
                            js = slice(32 * j, 32 * j + 32)
                            last = (j == 3) if X else (j == 0)
                            dcol = 32 * j + 31 if X else 32 * j
                            for hh in range(4):
                                h = hb * 4 + hh
                                hs = slice(hh * 128, (hh + 1) * 128)
                                po, pobuf = pos[hh]
                                E, qd = hE4[hh], hq4[hh]
                                S.op("pe", lambda e, po=po, js=js, j=j, h=h, last=last, qd=qd: e.matmul(po[js, 0:128], lhsT=qd[:, js], rhs=sbf[:, h * 128:(h + 1) * 128], start=False, stop=last, tile_position=(0, 32 * j)), reads=[B_hq4[hh], B_sbfh[h]], writes=[pobuf])
                                pk, pkbuf = ps("aux")
                                S.op("pe", lambda e, pk=pk, js=js, j=j, hs=hs, t=t, h=h: e.matmul(pk[:, 0:128], lhsT=h_kd[js, hs], rhs=v_own[js, t, h * 128:(h + 1) * 128], start=True, stop=True, tile_position=(32 * j, 0)), reads=[B_hkd, B_vown], writes=[pkbuf])
                                S.op("dve", lambda e, pk=pk, h=h, dcol=dcol, E=E: e.scalar_tensor_tensor(out=st[:, h * 128:(h + 1) * 128], in0=st[:, h * 128:(h + 1) * 128], scalar=E[:, dcol:dcol + 1], in1=pk[:, 0:128], op0=ALU.mult, op1=ALU.add), reads=[pkbuf, B_hE4[hh]], writes=[B_sth[stk][h]])
                                S.op("act", lambda e, h=h: e.copy(out=sbf[:, h * 128:(h + 1) * 128], in_=st[:, h * 128:(h + 1) * 128]), reads=[B_sth[stk][h]], writes=[B_sbfh[h]])
                        for hh in range(4):
                            h = hb * 4 + hh
                            po, pobuf = pos[hh]
                            if X:
                                S.op("act", lambda e, po=po, t=t, h=h: e.copy(out=o_acc[:, t, h * 128:(h + 1) * 128], in_=po[:, 0:128]), reads=[pobuf], writes=[B_oacc])
                            else:
                                S.op("dve", lambda e, po=po, t=t, h=h: e.tensor_tensor(out=o_acc[:, t, h * 128:(h + 1) * 128], in0=o_acc[:, t, h * 128:(h + 1) * 128], in1=po[:, 0:128], op=ALU.add), reads=[pobuf], writes=[B_oacc])

            chain("X")
            chain("Y")
            S.barrier()

            rr2 = Reg(r14_off)
            r_sg = T("r_sg", [128, 512], F32, rr2)
            r_sq = T("r_sq", [128, 512], F32, rr2)
            r_y = T("r_y", [128, 512], BF16, rr2)
            r_ss = T("r_ss", [128, 8], F32, rr2)
            B_rsg, B_rsq, B_ry, B_rss = [Buf(n) for n in ["rsg", "rsq", "ry", "rss"]]
            for hb in range(2):
                wt, wb = next_w(w_in, C_G + hb * 512)
                for t in range(NT):
                    pb, pbuf = proj_tok(hT, B_hT, t * 128, wt, wb, 0)
                    S.op("act", lambda e, pb=pb: e.activation(out=r_sg[:], in_=pb[:, :], func=AF.Silu), reads=[pbuf], writes=[B_rsg])
                    for hh in range(4):
                        hs = slice(hh * 128, (hh + 1) * 128)
                        oc = slice(hb * 512 + hh * 128, hb * 512 + (hh + 1) * 128)
                        S.op("act", lambda e, hs=hs, oc=oc, t=t, hh=hh: e.activation(out=r_sq[:, hs], in_=o_acc[:, t, oc], func=AF.Square, accum_out=r_ss[:, hh:hh + 1]), reads=[B_oacc], writes=[B_rsq, B_rss])
                    rstd_ops(r_ss[:, 0:4], B_rss, 1.0 / 128)
                    for hh in range(4):
                        hs = slice(hh * 128, (hh + 1) * 128)
                        oc = slice(hb * 512 + hh * 128, hb * 512 + (hh + 1) * 128)
                        S.op("dve", lambda e, hs=hs: e.tensor_tensor(out=r_sg[:, hs], in0=r_sg[:, hs], in1=ng_t[:, 0, :], op=ALU.mult), reads=[Bc], writes=[B_rsg])
                        S.op("dve", lambda e, hs=hs, oc=oc, t=t, hh=hh: e.scalar_tensor_tensor(out=r_y[:, hs], in0=o_acc[:, t, oc], scalar=r_ss[:, hh:hh + 1], in1=r_sg[:, hs], op0=ALU.mult, op1=ALU.mult), reads=[B_oacc, B_rss, B_rsg], writes=[B_ry])
                    p2, p2buf = ps("sm")
                    pv = p2[:, :].bitcast(BF16)

                    def f(e, pv=pv):
                        ins = None
                        for hh in range(4):
                            ins = e.transpose(out=pv[:, hh * 128:(hh + 1) * 128], in_=r_y[:, hh * 128:(hh + 1) * 128], identity=ident_b[:])
                        return ins
                    S.op("pe", f, reads=[B_ry, Bc], writes=[p2buf])
                    S.op("act", lambda e, pv=pv, hb=hb, t=t: e.copy(out=mixA[:, hb * 4:hb * 4 + 4, t * 128:(t + 1) * 128], in_=pv[:, 0:512].rearrange("p (h q) -> p h q", h=4)), reads=[p2buf], writes=[B_mixA])
            S.barrier()

            _stop_at("B")
            rC = Reg(oacc_off + 16384)
            x1t = T("x1t", [128, D], F32, rC)
            xin = T("xin", [128, D], F32, rC)
            gtb = T("gtb", [128, D], F32, Reg(R0 + 98304))
            gdiag = T("gdiag", [128, 128], F32, Reg(r14_off))
            B_x1t, B_xin, B_gdiag, B_gtb = Buf("x1t"), Buf("xin"), Buf("gdiag"), Buf("gtb")

            def gate_bcast(j0):
                for cb in range(4):
                    pb, pbuf = ps("big")
                    for jj in range(4):
                        j = cb * 4 + jj
                        S.op("dve", lambda e, j=j: e.tensor_scalar(out=gdiag[:], in0=ident_f[:], scalar1=modT[:, j0 + j, 0:1], scalar2=None, op0=ALU.mult), reads=[B_modT, Bc], writes=[B_gdiag])
                        S.op("pe", lambda e, pb=pb, jj=jj: e.matmul(pb[:, jj * 128:(jj + 1) * 128], lhsT=ones_f[:], rhs=gdiag[:], start=True, stop=True), reads=[B_gdiag, Bc], writes=[pbuf])
                    S.op("act", lambda e, pb=pb, cb=cb: e.copy(out=gtb[:, cb * 512:(cb + 1) * 512], in_=pb[:, :]), reads=[pbuf], writes=[B_gtb])

            gate_bcast(32)
            wo_offs = [R0, R0 + 16384, vown_off, oacc_off]
            wo_slots = [T("wo%d" % i, [128, KT, 512], BF16, Reg(wo_offs[i])) for i in range(4)]
            B_wo = [Buf("wo%d" % i) for i in range(4)]
            for cb in range(4):
                load_w(wo_slots[cb][:], B_wo[cb], "wo%d" % cb, w_out, cb * 512, 512)
            for t in range(NT):
                S.dma("sp", "xin", xin[:], xo[t * 128:(t + 1) * 128, :], writes=[B_xin])
                for cb in range(4):
                    pb, pbuf = ps("big")

                    def f(e, pb=pb, cb=cb, t=t):
                        ins = None
                        for j in range(KT):
                            src = mixA if j < 8 else mixB
                            ins = e.matmul(pb[:, :], lhsT=src[:, j % 8, t * 128:(t + 1) * 128], rhs=wo_slots[cb][:, j, :], start=(j == 0), stop=(j == KT - 1))
                        return ins
                    S.op("pe", f, reads=[B_mixA, B_mixB, B_wo[cb]], writes=[pbuf])
                    cs = slice(cb * 512, (cb + 1) * 512)
                    S.op("dve", lambda e, pb=pb, cs=cs: e.tensor_tensor(out=x1t[:, cs], in0=pb[:, :], in1=gtb[:, cs], op=ALU.mult), reads=[pbuf, B_gtb], writes=[B_x1t])
                    S.op("pool", lambda e, cs=cs: e.tensor_tensor(out=x1t[:, cs], in0=x1t[:, cs], in1=xin[:, cs], op=ALU.add), reads=[B_xin], writes=[B_x1t])
                S.dma("sp", "x1o", y[t * 128:(t + 1) * 128, :], x1t[:], reads=[B_x1t], writes=[], is_out=True)
            S.barrier()
            h2T = T("h2T", [128, KT, T_OWN], BF16, Reg(R0))
            B_h2T = Buf("h2T")
            nC = NormCtx(Reg(R0 + 32768), "C")
            for t in range(NT):
                nC.from_dram(y[t * 128:(t + 1) * 128, :], h2T, B_h2T, t * 128, "ffn")
            S.barrier()

            aT = T("aT", [128, NFB, T_OWN], BF16, Reg(R0 + 32768))
            B_aT = Buf("aT")
            rF = Reg(R0 + 32768 + 90112)
            wg = T("wg", [128, KT, 512], BF16, rF)
            wu = T("wu", [128, KT, 512], BF16, rF)
            f_s = T("f_s", [128, 512], F32, rF)
            assert rF.o <= LIMIT, rF.o
            B_wg, B_wu, B_fs = Buf("wg"), Buf("wu"), Buf("fs")
            for gb in range(11):
                load_w(wg[:], B_wg, "wg", w_gu, gb * 512, 512)
                load_w(wu[:], B_wu, "wu", w_gu, DFF + gb * 512, 512)
                for jj in range(4):
                    fb = gb * 4 + jj
                    for half in range(2):
                        pg, pgbuf = ps("big")
                        pu, pubuf = ps("big")

                        def f(e, pg=pg, jj=jj, half=half, w=wg):
                            ins = None
                            for kt in range(KT):
                                ins = e.matmul(pg[:, :], lhsT=w[:, kt, jj * 128:(jj + 1) * 128], rhs=h2T[:, kt, half * 512:(half + 1) * 512], start=(kt == 0), stop=(kt == KT - 1))
                            return ins

                        def f2(e, pu=pu, jj=jj, half=half, w=wu):
                            ins = None
                            for kt in range(KT):
                                ins = e.matmul(pu[:, :], lhsT=w[:, kt, jj * 128:(jj + 1) * 128], rhs=h2T[:, kt, half * 512:(half + 1) * 512], start=(kt == 0), stop=(kt == KT - 1))
                            return ins
                        S.op("pe", f, reads=[B_h2T, B_wg], writes=[pgbuf])
                        S.op("pe", f2, reads=[B_h2T, B_wu], writes=[pubuf])
                        S.op("act", lambda e, pg=pg: e.activation(out=f_s[:], in_=pg[:, :], func=AF.Silu), reads=[pgbuf], writes=[B_fs])
                        S.op("dve", lambda e, pu=pu, fb=fb, half=half: e.tensor_tensor(out=aT[:, fb, half * 512:(half + 1) * 512], in0=pu[:, :], in1=f_s[:], op=ALU.mult), reads=[pubuf, B_fs], writes=[B_aT])
            S.barrier()

            h2_free = Reg(R0)
            gtb2 = T("gtb2", [128, D], F32, h2_free)
            gdiag2 = T("gdiag2", [128, 128], F32, h2_free)
            x1c = [T("x1c%d" % i, [128, 512], F32, h2_free) for i in range(2)]
            oc_t = [T("oc%d" % i, [128, 512], F32, h2_free) for i in range(2)]
            assert h2_free.o <= R0 + 32768
            wd = T("wd", [128, NFB, 512], BF16, Reg(R0 + 32768 + 90112))
            assert R0 + 32768 + 90112 + 45056 <= LIMIT
            B_wd = Buf("wd")
            B_x1c, B_oc = [Buf("x1c0"), Buf("x1c1")], [Buf("oc0"), Buf("oc1")]
            B_gtb2, B_gdiag2 = Buf("gtb2"), Buf("gdiag2")
            for cb in range(4):
                pb, pbuf = ps("big")
                for jj in range(4):
                    j = cb * 4 + jj
                    S.op("dve", lambda e, j=j: e.tensor_scalar(out=gdiag2[:], in0=ident_f[:], scalar1=modT[:, 80 + j, 0:1], scalar2=None, op0=ALU.mult), reads=[B_modT, Bc], writes=[B_gdiag2])
                    S.op("pe", lambda e, pb=pb, jj=jj: e.matmul(pb[:, jj * 128:(jj + 1) * 128], lhsT=ones_f[:], rhs=gdiag2[:], start=True, stop=True), reads=[B_gdiag2, Bc], writes=[pbuf])
                S.op("act", lambda e, pb=pb, cb=cb: e.copy(out=gtb2[:, cb * 512:(cb + 1) * 512], in_=pb[:, :]), reads=[pbuf], writes=[B_gtb2])
            cnt = 0
            for cb in range(4):
                S.dma("pool", "wd", wd[:], w_dn[:, cb * 512:(cb + 1) * 512].rearrange("(kt p) n -> p kt n", p=128), writes=[B_wd])
                for t in range(NT):
                    i = cnt % 2
                    cnt += 1
                    pb, pbuf = ps("big")

                    def f(e, pb=pb, t=t):
                        ins = None
                        for fb in range(NFB):
                            ins = e.matmul(pb[:, :], lhsT=aT[:, fb, t * 128:(t + 1) * 128], rhs=wd[:, fb, :], start=(fb == 0), stop=(fb == NFB - 1))
                        return ins
                    S.op("pe", f, reads=[B_aT, B_wd], writes=[pbuf])
                    S.dma("sp", "x1c%d" % i, x1c[i][:], y[t * 128:(t + 1) * 128, cb * 512:(cb + 1) * 512], writes=[B_x1c[i]])
                    S.op("dve", lambda e, pb=pb, i=i, cb=cb: e.tensor_tensor(out=oc_t[i][:], in0=pb[:, :], in1=gtb2[:, cb * 512:(cb + 1) * 512], op=ALU.mult), reads=[pbuf, B_gtb2], writes=[B_oc[i]])
                    S.op("pool", lambda e, i=i: e.tensor_tensor(out=oc_t[i][:], in0=oc_t[i][:], in1=x1c[i][:], op=ALU.add), reads=[B_x1c[i]], writes=[B_oc[i]])
                    S.dma("sp", "yo%d" % i, y[t * 128:(t + 1) * 128, cb * 512:(cb + 1) * 512], oc_t[i][:], reads=[B_oc[i]], writes=[], is_out=True)
            S.barrier()


        except _Stop:
            S.barrier()
        with nc.Block() as block:
            S.emit(block)
    return nc


_NC_CACHE = {}


def _rope_tables(tpos):
    nf = 32
    inv = (10000.0 ** (-np.arange(nf, dtype=np.float32) / nf)).astype(np.float32)
    row = (tpos // 64).astype(np.float32)
    col = (tpos % 64).astype(np.float32)
    ar = row[:, None] * inv[None, :]
    ac = col[:, None] * inv[None, :]
    cr, sr, cc, sc = np.cos(ar), np.sin(ar), np.cos(ac), np.sin(ac)
    C = np.concatenate([cr, cr, cc, cc], axis=1)
    Sg = np.concatenate([-sr, sr, -sc, sc], axis=1)
    return np.concatenate([C, Sg], axis=1).astype(np.float32)


def _const_mats():
    i = np.arange(128)
    r, s = i[:, None], i[None, :]
    same = (r // 32) == (s // 32)
    cm = np.zeros((8, 128, 128), np.float32)
    cm[0] = same & (r <= s)
    cm[1] = same & (r >= s)
    cm[2] = r > s
    cm[3] = r < s
    cm[4] = same & (r > s)
    cm[5] = same & (r < s)
    cm[6] = np.eye(128)
    return cm


def kernel(x, c, ctx, c_ctx, w_mod, b_mod, norm_mix_g, norm_ffn_g, w_in, hg_lb,
           hg_norm_g, q_norm_g, k_norm_g, attn_sink, w_out, w_gate_up, w_down):
    f32 = lambda a: np.ascontiguousarray(np.asarray(a, dtype=np.float32))
    x, c, ctx, c_ctx = f32(x), f32(c), f32(ctx), f32(c_ctx)
    w_in0 = f32(w_in)[0]
    hg_lb = f32(hg_lb)
    if "nc" not in _NC_CACHE:
        _NC_CACHE["nc"] = build_program()
    nc = _NC_CACHE["nc"]

    cq, cff, cfb, ci, cg = [w_in0[:, k * 1024:(k + 1) * 1024] for k in range(5)]
    catt = w_in0[:, 5120:]
    catq, catkv = catt[:, :1024], catt[:, 1024:1536]
    w_in_var = {}
    for rev in (0, 1):
        for m in (0, 1):
            fX, fY = (cfb, cff) if rev else (cff, cfb)
            fS1 = fY if m else fX
            w_in_var[(rev, m)] = np.ascontiguousarray(np.concatenate([catq, catkv, cq, ci, fX, fY, cg, fS1], axis=1))
    shared = {
        "w_mod": f32(w_mod)[0], "b_mod": f32(b_mod).reshape(1, -1),
        "w_out": f32(w_out)[0], "w_gu": f32(w_gate_up)[0], "w_dn": f32(w_down)[0],
        "gvec": np.ascontiguousarray(np.concatenate([f32(norm_mix_g)[0].reshape(16, 128).T, f32(norm_ffn_g)[0].reshape(16, 128).T], axis=1)),
        "ngv": np.concatenate([f32(hg_norm_g)[0], f32(q_norm_g)[0], f32(k_norm_g)[0]]).reshape(1, -1),
        "sinkv": f32(attn_sink).reshape(1, 8),
        "cmat": _const_mats(),
    }
    i = np.arange(128)
    prevm = np.tile((i[:, None] >= i[None, :]).astype(np.float32), (1, 4))
    nextm = np.tile((i[:, None] <= i[None, :]).astype(np.float32), (1, 4))
    in_maps = []
    meta = []
    for core in range(8):
        b, q = core // 4, core % 4
        rev = 1 if q >= 2 else 0
        m = 1 if q in (0, 3) else 0
        xb = x[b]
        t0 = 1024 * q
        own_idx = np.arange(t0, t0 + 1024)
        if rev:
            own_idx = own_idx[::-1]
        if not rev:
            hl = np.arange(t0 - 128, t0)
            hr = np.arange(t0 + 1024, t0 + 1152)
        else:
            hl = np.arange(t0 + 1024, t0 + 1152)[::-1]
            hr = np.arange(t0 - 128, t0)[::-1]
        valid = lambda idx: (idx >= 0) & (idx < 4096)
        def gather(idx):
            out = np.zeros((len(idx), D), np.float32)
            v = valid(idx)
            out[v] = xb[idx[v]]
            return out
        cx = ctx[b][::-1] if rev else ctx[b]
        xhc = np.concatenate([gather(hl), gather(hr), cx], axis=0)
        def quarter(j, reverse):
            idx = np.arange(1024 * j, 1024 * j + 1024)
            return idx[::-1] if reverse else idx
        dX_fwd = not rev
        def prefix(fwd):
            return [quarter(j, False) for j in range(0, q)] if fwd else [quarter(j, True) for j in range(3, q, -1)]
        Xl, Yl = prefix(dX_fwd), prefix(not dX_fwd)
        slots = Yl if m else [Xl[0]] + Yl
        assert len(slots) == 3
        xs = np.concatenate([xb[s] for s in slots], axis=0)
        dirX = 1 if rev else 0
        dirY = 1 - dirX
        dirS1 = dirY if m else dirX
        lbp = np.stack([hg_lb[:, d, :].reshape(-1) for d in (dirX, dirY, dirS1)], axis=0)
        pres = np.concatenate([hl, own_idx, hr])
        rope = _rope_tables(np.clip(pres, 0, 4095))
        am = np.stack([np.zeros_like(prevm) if m else prevm, prevm, nextm], axis=0)
        cT = np.stack([c[b].reshape(16, 128).T, c_ctx.reshape(16, 128).T], axis=2).reshape(128, 32)
        flg = np.zeros((128, 2), np.float32)
        flg[:, 0] = m
        flg[:, 1] = 1 - m
        d = dict(shared)
        d.update({
            "xo": np.ascontiguousarray(xb[own_idx]), "xs": np.ascontiguousarray(xs), "xhc": np.ascontiguousarray(xhc),
            "cT": np.ascontiguousarray(cT.astype(np.float32)), "w_in_r": w_in_var[(rev, m)],
            "lbp": np.ascontiguousarray(lbp), "amask": np.ascontiguousarray(am), "rope": rope, "flg": flg,
        })
        in_maps.append(d)
        meta.append((b, own_idx))
    res = run_bass_kernel_spmd(nc, in_maps, core_ids=list(range(8)))
    out = np.zeros((2, 4096, D), np.float32)
    for core in range(8):
        b, own_idx = meta[core]
        out[b, own_idx] = np.asarray(res.results[core]["y"], dtype=np.float32)
    return out
```
